# Optimizing a Trainium2 kernel written in Bass

```python
import jax, jax.numpy as jnp
from jax import lax
import numpy as np

D_MODEL = 1024
BATCH = 8
SEQ = 4096
DEPTH = 1

CHUNK = 64
Q_BLOCK = 128
MIX_WIDTH = D_MODEL
POOL_WIDTH = MIX_WIDTH // 2
POOL_WINDOWS = (2, 4, 8, 16)
POOL_GROUPS = len(POOL_WINDOWS)
POOL_GROUP_DIM = POOL_WIDTH // POOL_GROUPS
ATTN_WIDTH = MIX_WIDTH - POOL_WIDTH
DIFF_HEAD_DIM = 64
V_HEAD_DIM = 2 * DIFF_HEAD_DIM
N_DIFF_HEADS = ATTN_WIDTH // V_HEAD_DIM
IN_WIDTH = POOL_WIDTH + 3 * ATTN_WIDTH
D_FF = 2816
CONV_WIDTH = 3
ROPE_THETA = 10000.0
EPS = 1e-6

kernel_name = "hybrid_pool_diffattn_convffn_block"


def rmsnorm(x, g):
    xf = x.astype(jnp.float32)
    y = xf * lax.rsqrt(jnp.mean(xf * xf, axis=-1, keepdims=True) + EPS)
    return (y * g.astype(jnp.float32)).astype(x.dtype)


def rope_tables(positions, dtype):
    inv_freq = ROPE_THETA ** (-jnp.arange(0, DIFF_HEAD_DIM, 2, dtype=jnp.float32) / DIFF_HEAD_DIM)
    ang = positions.astype(jnp.float32)[..., None] * inv_freq
    return (jnp.cos(ang)[:, :, None, None, :].astype(dtype),
            jnp.sin(ang)[:, :, None, None, :].astype(dtype))


def apply_rope(t, cos, sin):
    t1, t2 = jnp.split(t, 2, axis=-1)
    return jnp.concatenate([t1 * cos - t2 * sin, t1 * sin + t2 * cos], axis=-1)


def multiscale_pool(p, pool_w, pool_scale):
    s = p.shape[1]
    pf = p.astype(jnp.float32)
    csum = jnp.pad(jnp.cumsum(pf, axis=1), ((0, 0), (1, 0), (0, 0)))
    t = jnp.arange(s)
    pooled = []
    for g, w in enumerate(POOL_WINDOWS):
        cs = csum[..., g * POOL_GROUP_DIM:(g + 1) * POOL_GROUP_DIM]
        upper = cs[:, 1:]
        lower = jnp.pad(cs[:, :s + 1 - w], ((0, 0), (w - 1, 0), (0, 0)))
        count = jnp.minimum(t + 1, w).astype(jnp.float32)[None, :, None]
        pooled.append((upper - lower) / count)
    d = (jnp.concatenate(pooled, axis=-1) - pf).astype(p.dtype)
    d = d.reshape(p.shape[0], s, POOL_GROUPS, POOL_GROUP_DIM)
    y = jnp.einsum('bsgc,gcd->bsgd', d, pool_w).reshape(p.shape[0], s, POOL_WIDTH)
    return y * pool_scale


def diff_attention(q, k, v, lam, lam_init, head_g):
    b, h, _, s, _ = q.shape
    scale = DIFF_HEAD_DIM ** -0.5
    outs = []
    for i in range(s // Q_BLOCK):
        q0 = i * Q_BLOCK
        end = q0 + Q_BLOCK
        qb = q[:, :, :, q0:end]
        kb = k[:, :, :, :end]
        vb = v[:, :, :end]
        scores = jnp.einsum('bhnqd,bhnkd->bhnqk', qb, kb).astype(jnp.float32) * scale
        q_chunk = (q0 + jnp.arange(Q_BLOCK)) // CHUNK
        k_chunk = jnp.arange(end) // CHUNK
        mask = k_chunk[None, :] <= q_chunk[:, None]
        probs = jax.nn.softmax(jnp.where(mask, scores, -jnp.inf), axis=-1)
        attn = probs[:, :, 0] - lam * probs[:, :, 1]
        outs.append(jnp.einsum('bhqk,bhkd->bhqd', attn.astype(v.dtype), vb))
    o = jnp.concatenate(outs, axis=2)
    o = rmsnorm(o, head_g[None, :, None, :]) * (1.0 - lam_init)
    return o.transpose(0, 2, 1, 3).reshape(b, s, ATTN_WIDTH)


def causal_dwconv(u, w, bias):
    s = u.shape[1]
    upad = jnp.pad(u, ((0, 0), (CONV_WIDTH - 1, 0), (0, 0)))
    out = bias
    for j in range(CONV_WIDTH):
        out = out + w[j] * upad[:, j:j + s]
    return out


def setup_inputs(seed: int = 0) -> dict:
    key = jax.random.key(seed)
    ks = jax.random.split(key, 20)
    nrm = lambda k, shape, s: jax.random.normal(k, shape, jnp.float32) * s
    x = jax.random.normal(ks[0], (BATCH, SEQ, D_MODEL), jnp.float32)
    offset = jax.random.randint(ks[1], (BATCH,), 0, 10000, dtype=jnp.int32)
    positions = offset[:, None] + jnp.arange(SEQ, dtype=jnp.int32)[None, :]
    return {
        "x": x,
        "positions": positions,
        "norm_mix_g": 1.0 + nrm(ks[2], (DEPTH, D_MODEL), 0.05),
        "w_in": nrm(ks[3], (DEPTH, D_MODEL, IN_WIDTH), D_MODEL ** -0.5),
        "pool_w": nrm(ks[4], (DEPTH, POOL_GROUPS, POOL_GROUP_DIM, POOL_GROUP_DIM), POOL_GROUP_DIM ** -0.5),
        "pool_scale": 1.0 + nrm(ks[5], (DEPTH, POOL_WIDTH), 0.05),
        "lambda_q1": nrm(ks[6], (DEPTH, DIFF_HEAD_DIM), 0.1),
        "lambda_k1": nrm(ks[7], (DEPTH, DIFF_HEAD_DIM), 0.1),
        "lambda_q2": nrm(ks[8], (DEPTH, DIFF_HEAD_DIM), 0.1),
        "lambda_k2": nrm(ks[9], (DEPTH, DIFF_HEAD_DIM), 0.1),
        "attn_norm_g": 1.0 + nrm(ks[10], (DEPTH, N_DIFF_HEADS, V_HEAD_DIM), 0.05),
        "w_o": nrm(ks[11], (DEPTH, MIX_WIDTH, D_MODEL), MIX_WIDTH ** -0.5),
        "norm_ffn_g": 1.0 + nrm(ks[12], (DEPTH, D_MODEL), 0.05),
        "w_up": nrm(ks[13], (DEPTH, D_MODEL, 2 * D_FF), D_MODEL ** -0.5),
        "conv_w": nrm(ks[14], (DEPTH, CONV_WIDTH, 2 * D_FF), CONV_WIDTH ** -0.5),
        "conv_b": nrm(ks[15], (DEPTH, 2 * D_FF), 0.02),
        "w_down": nrm(ks[16], (DEPTH, D_FF, D_MODEL), D_FF ** -0.5),
        "norm_final_g": 1.0 + nrm(ks[17], (D_MODEL,), 0.05),
    }


def reference(x, positions, norm_mix_g, w_in, pool_w, pool_scale, lambda_q1, lambda_k1,
              lambda_q2, lambda_k2, attn_norm_g, w_o, norm_ffn_g, w_up, conv_w, conv_b,
              w_down, norm_final_g):
    b, s, _ = x.shape
    cos, sin = rope_tables(positions, x.dtype)
    for l in range(DEPTH):
        h = rmsnorm(x, norm_mix_g[l])
        proj = h @ w_in[l]
        p, q, k, v = jnp.split(
            proj, [POOL_WIDTH, POOL_WIDTH + ATTN_WIDTH, POOL_WIDTH + 2 * ATTN_WIDTH], axis=-1)
        pool_out = multiscale_pool(p, pool_w[l], pool_scale[l])
        q = apply_rope(q.reshape(b, s, N_DIFF_HEADS, 2, DIFF_HEAD_DIM), cos, sin)
        k = apply_rope(k.reshape(b, s, N_DIFF_HEADS, 2, DIFF_HEAD_DIM), cos, sin)
        q = q.transpose(0, 2, 3, 1, 4)
        k = k.transpose(0, 2, 3, 1, 4)
        v = v.reshape(b, s, N_DIFF_HEADS, V_HEAD_DIM).transpose(0, 2, 1, 3)
        lam_init = 0.8 - 0.6 * float(np.exp(-0.3 * l))
        lam = (jnp.exp(jnp.sum(lambda_q1[l].astype(jnp.float32) * lambda_k1[l].astype(jnp.float32)))
               - jnp.exp(jnp.sum(lambda_q2[l].astype(jnp.float32) * lambda_k2[l].astype(jnp.float32)))
               + lam_init)
        attn_out = diff_attention(q, k, v, lam, lam_init, attn_norm_g[l])
        mix = jnp.concatenate([pool_out, attn_out], axis=-1) @ w_o[l]
        x = x + mix
        h = rmsnorm(x, norm_ffn_g[l])
        u = causal_dwconv(h @ w_up[l], conv_w[l], conv_b[l])
        gate, val = jnp.split(u, 2, axis=-1)
        x = x + (jax.nn.silu(gate) * val) @ w_down[l]
    return rmsnorm(x, norm_final_g)
```

```python
import math
from contextlib import ExitStack
import numpy as np
import concourse.bass as bass
import concourse.mybir as mybir
from concourse.bass_utils import run_bass_kernel_spmd

F32 = mybir.dt.float32
BF16 = mybir.dt.bfloat16
I32 = mybir.dt.int32
AF = mybir.ActivationFunctionType
ALU = mybir.AluOpType

D = 1024
S = 4096
T = 256
NB = S // T
KC = D // 128
DFF = 2816
FC = DFF // 128
NH = 4
EPS = 1e-6
LAM_INIT = 0.8 - 0.6 * float(np.exp(-0.3 * 0))
TWO_PI = 2.0 * math.pi
C1 = 6.28125
C2 = TWO_PI - C1

P_GMIX, P_GFFN, P_GFIN = 0, 8, 16
P_PSC, P_HG = 24, 28
P_CW0, P_CW1, P_CW2, P_CB = 32, 76, 120, 164
P_INVF, P_SGN = 208, 209
NPRM = 210
C_BD0, C_BD, C_BOFF, C_PERM = 0, 4, 8, 12
NCST = 13


class Chan:
    def __init__(self, sem, inc, name=None):
        self.sem, self.inc, self.cnt, self.name = sem, inc, 0, name


class Tok:
    __slots__ = ("w", "r", "excl")

    def __init__(self, excl=False):
        self.w = None
        self.r = {}
        self.excl = excl


class Sync:
    def __init__(self, nc, es, marks=None):
        self.nc, self.es = nc, es
        self.marks = marks
        self.needed = {}
        self.eng = {}
        for name, obj in (("pe", nc.tensor), ("act", nc.scalar), ("dve", nc.vector),
                          ("pool", nc.gpsimd), ("sp", nc.sync)):
            ch = Chan(es.enter_context(nc.semaphore("s_" + name)), 1, name)
            self.eng[name] = [obj, ch, {}]
            self.needed[name] = set()
        self.nchan = 0
        self.markset = {k: set(v) for k, v in marks.items()} if marks is not None else None
        self.allchans = [self.eng[k][1] for k in self.eng]

    def dma_chan(self):
        self.nchan += 1
        ch = Chan(self.es.enter_context(self.nc.semaphore("d%d" % self.nchan)), 16)
        self.allchans.append(ch)
        return ch

    def _wait(self, obj, ch, v):
        if ch.name is None:
            obj.wait_ge(ch.sem, v)
        elif self.marks is None:
            self.needed[ch.name].add(v)
            obj.wait_ge(ch.sem, v)
        else:
            import bisect
            m = self.marks[ch.name]
            r = bisect.bisect_right(m, v)
            assert r > 0 and m[r - 1] == v, (ch.name, v)
            obj.wait_ge(ch.sem, r)

    def op(self, en, emit, rd=(), wr=(), chan=None, n=1):
        obj, ech, waited = self.eng[en]
        need = {}

        def add(cv, raw):
            if cv is None:
                return
            ch, v = cv
            if ch is ech and en == "pe":
                return
            if need.get(ch, 0) < v:
                need[ch] = v

        for t in rd:
            add(t.w, True)
            if t.excl:
                for ch, v in t.r.items():
                    if ch is not ech:
                        add((ch, v), False)
        for t in wr:
            add(t.w, False)
            for ch, v in t.r.items():
                add((ch, v), False)
        for ch, v in need.items():
            if waited.get(ch, 0) < v:
                self._wait(obj, ch, v)
                waited[ch] = v
        if chan == "new":
            chan = self.dma_chan()
        c = chan if chan is not None else ech
        res = emit(obj)
        if n == 1 and not isinstance(res, (list, tuple)):
            res = [res]
        assert len(res) == n
        c.cnt += c.inc * n
        if c.name is None or self.marks is None or c.cnt in self.markset[c.name]:
            for ins in res:
                ins.then_inc(c.sem, c.inc)
        for t in wr:
            t.w = (c, c.cnt)
            t.r = {}
        for t in rd:
            if t.r.get(c, 0) < c.cnt:
                t.r[c] = c.cnt
        return c.cnt

    def wait_all(self, en, chans):
        obj, ech, waited = self.eng[en]
        for ch in chans:
            if ch.cnt > 0 and waited.get(ch, 0) < ch.cnt:
                self._wait(obj, ch, ch.cnt)
                waited[ch] = ch.cnt


class _Stop(Exception):
    pass


def build_nc(nblk=NB, dbg=None, stop=None):
    marks = _build(nblk, dbg, stop, None)
    return _build(nblk, dbg, stop, marks)


def _build(nblk, dbg, stop, marks):
    def st(n):
        if stop is not None and n >= stop:
            raise _Stop()
    nc = bass.Bass("TRN2", target_bir_lowering=False)
    xTd = nc.dram_tensor("xT", [D, S], F32, kind="ExternalInput").ap()
    posd = nc.dram_tensor("pos", [128, S], I32, kind="ExternalInput").ap()
    wid = nc.dram_tensor("wi", [128, KC * 2048], F32, kind="ExternalInput").ap()
    wod = nc.dram_tensor("wo", [128, KC * 1024], F32, kind="ExternalInput").ap()
    pwd = nc.dram_tensor("pw", [128, 512], F32, kind="ExternalInput").ap()
    wud = nc.dram_tensor("wu", [128, 44 * KC * 128], F32, kind="ExternalInput").ap()
    wdd = nc.dram_tensor("wd", [128, FC * 1024], F32, kind="ExternalInput").ap()
    prmd = nc.dram_tensor("prm", [128, NPRM], F32, kind="ExternalInput").ap()
    lamd = nc.dram_tensor("lamv", [128, 256], F32, kind="ExternalInput").ap()
    cstd = nc.dram_tensor("cst", [128, NCST * 128], F32, kind="ExternalInput").ap()
    outd = nc.dram_tensor("outT", [D, S], F32, kind="ExternalOutput").ap()
    x1d = nc.dram_tensor("x1s", [D, S], F32, kind="Internal").ap()
    xTv = xTd.rearrange("(c p) t -> p c t", p=128)
    x1v = x1d.rearrange("(c p) t -> p c t", p=128)
    outv = outd.rearrange("(c p) t -> p c t", p=128)
    dbgd = None
    if dbg is not None:
        dbgd = nc.dram_tensor("dbg", [128, dbg], F32, kind="ExternalOutput").ap()

    with ExitStack() as es:
        sy = Sync(nc, es, marks)
        op = sy.op

        def sb(name, shape, dt, st=es):
            return st.enter_context(nc.sbuf_tensor("t_" + name, shape, dt))

        prm = sb("prm", [128, NPRM], F32)
        ones_b = sb("ones_b", [128, 128], BF16)
        xT = [sb("xT%d" % i, [128, KC, T], F32) for i in range(2)]
        xT_t = [Tok() for _ in range(2)]
        xld = [sy.dma_chan() for _ in range(2)]
        xst = [sy.dma_chan() for _ in range(2)]
        sq = sb("sq", [128, KC, T], BF16)
        sq_t = Tok()
        rstd = sb("rstd", [128, T], F32)
        rstd_t = Tok()
        hT = sb("hT", [128, KC, T], BF16)
        hT_t = Tok()
        NT = 12
        tmp = [sb("tmp%d" % i, [128, T + 2], F32) for i in range(NT)]
        tmp_t = [Tok() for _ in range(NT)]
        tmp_i = [0]

        def gettmp():
            i = tmp_i[0] % NT
            tmp_i[0] += 1
            return tmp[i], tmp_t[i]

        banks = [es.enter_context(nc.psum_tensor("bk%d" % i, [128, 512], F32)) for i in range(8)]
        bank_t = [Tok(excl=True) for _ in range(8)]
        bank_i = [0]
        held = set()

        def getbank(hold=False):
            while True:
                i = bank_i[0] % 8
                bank_i[0] += 1
                if i not in held:
                    break
            if hold:
                held.add(i)
            return banks[i], bank_t[i], i

        prm_t = Tok()
        ones_t = Tok()
        cch = sy.dma_chan()
        op("sp", lambda e: e.dma_start(out=prm[:], in_=prmd[:, :]), wr=[prm_t], chan=cch)
        op("dve", lambda e: e.memset(ones_b[:], 1.0), wr=[ones_t])
        x1_t = [Tok() for _ in range(NB)]

        def pcol(c):
            return prm[:, c:c + 1]

        def norm_stats(xb, xb_t):
            op("act", lambda e: e.activation(out=sq[:], in_=xb[:], func=AF.Square),
               rd=[xb_t], wr=[sq_t])
            bk, bt, _ = getbank()
            for c in range(KC):
                op("pe", lambda e, c=c: e.matmul(bk[:, 0:T], lhsT=ones_b[:], rhs=sq[:, c, :],
                                                 start=(c == 0), stop=(c == KC - 1)),
                   rd=[ones_t, sq_t], wr=[bt])
            t1, t1t = gettmp()
            op("act", lambda e: e.activation(out=t1[:, 0:T], in_=bk[:, 0:T], func=AF.Ln,
                                             scale=1.0 / D, bias=EPS),
               rd=[bt], wr=[t1t])
            op("act", lambda e: e.activation(out=rstd[:], in_=t1[:, 0:T], func=AF.Exp, scale=-0.5),
               rd=[t1t], wr=[rstd_t])

        def norm_apply(xb, xb_t, gcol, dst, dst_t):
            for c in range(KC):
                op("dve", lambda e, c=c: e.scalar_tensor_tensor(
                    out=dst[:, c, :], in0=xb[:, c, :], scalar=pcol(gcol + c), in1=rstd[:],
                    op0=ALU.mult, op1=ALU.mult),
                   rd=[xb_t, rstd_t, prm_t], wr=[dst_t])

        with ExitStack() as p1:
            def sb1(name, shape, dt):
                return sb(name, shape, dt, p1)
            KT = sb1("KT", [128, NH, S], BF16)
            KT_t = [[Tok() for _ in range(NB)] for _ in range(NH)]
            V = sb1("V", [128, S // 128, 512], BF16)
            V_t = [Tok() for _ in range(S // 128)]
            Wi = sb1("Wi", [128, KC, 2048], BF16)
            Wi_t = [Tok() for _ in range(KC)]
            Wo = sb1("Wo", [128, KC, 1024], BF16)
            Wo_t = [Tok() for _ in range(KC)]
            Pw = sb1("Pw", [128, 4, 128], BF16)
            Pw_t = Tok()
            cst = sb1("cst", [128, NCST, 128], BF16)
            cst_t = Tok()
            lt = sb1("lt", [128, 256], F32)
            lt2 = sb1("lt2", [128, 128], F32)
            rrb = [sb1("rrb%d" % i, [128, 2 * T], F32) for i in range(2)]
            rrb_t = [Tok() for _ in range(2)]
            osqb = sb1("osqb", [128, T], BF16)
            osqb_t = Tok()
            lt_t = Tok()
            posi = sb1("posi", [128, T], I32)
            posi_t = Tok()
            ki = sb1("ki", [128, T], I32)
            ki_t = Tok()
            tabs = [sb1("tab%d" % i, [128, T], F32) for i in range(4)]
            tab_t = Tok()
            catT = sb1("catT", [128, KC, T], BF16)
            cat_t = [Tok() for _ in range(KC)]
            qm = [sb1("qm%d" % i, [128, NH, T], BF16) for i in range(2)]
            qT_t = [Tok() for _ in range(NH)]
            ptok = sb1("ptok", [128, 3, 512], BF16)
            ptok_t = [Tok() for _ in range(3)]
            dT = sb1("dT", [128, 4, T], BF16)
            dT_t = [Tok() for _ in range(4)]
            raw = [sb1("raw%d" % i, [128, T], BF16) for i in range(2)]
            raw_t = [Tok() for _ in range(2)]
            Pb = [sb1("Pb%d" % i, [128, 2, T], BF16) for i in range(3)]
            Pb_t = [Tok() for _ in range(3)]
            pch = sy.dma_chan()

            op("sp", lambda e: e.dma_start(out=lt[:], in_=lamd[:, :]), wr=[lt_t], chan=cch)
            prm_t.w = (cch, cch.cnt)
            lt_t.w = (cch, cch.cnt)
            wch = sy.dma_chan()
            wiv = wid.rearrange("p (k n) -> p k n", k=KC)
            wov = wod.rearrange("p (k n) -> p k n", k=KC)
            for k in range(KC):
                op("pool", lambda e, k=k: e.dma_start(out=Wi[:, k, :], in_=wiv[:, k, :]),
                   wr=[Wi_t[k]], chan=wch)
            op("pool", lambda e: e.dma_start(out=Pw[:], in_=pwd.rearrange("p (g n) -> p g n", g=4)),
               wr=[Pw_t], chan=wch)
            for k in range(KC):
                op("pool", lambda e, k=k: e.dma_start(out=Wo[:, k, :], in_=wov[:, k, :]),
                   wr=[Wo_t[k]], chan=wch)
            op("pool", lambda e: e.dma_start(out=cst[:], in_=cstd.rearrange("p (a b) -> p a b", a=NCST)),
               wr=[cst_t], chan=wch)
            for _t in Wi_t + Wo_t + [Pw_t, cst_t]:
                _t.w = (wch, wch.cnt)
            op("dve", lambda e: e.memset(ptok[:, 0, :], 0.0), wr=[ptok_t[0]])
            op("dve", lambda e: e.memset(qm[0][:], 0.0), wr=qT_t)
            op("dve", lambda e: e.memset(qm[1][:], 0.0), wr=qT_t)
            op("dve", lambda e: e.tensor_tensor(out=lt2[:, 0:128], in0=lt[:, 0:128],
                                                in1=lt[:, 128:256], op=ALU.mult),
               rd=[lt_t], wr=[lt_t])
            op("dve", lambda e: e.tensor_reduce(out=lt[:, 0:2],
                                                in_=lt2[:, 0:128].rearrange("p (a b) -> p a b", a=2),
                                                axis=mybir.AxisListType.X, op=ALU.add),
               rd=[lt_t], wr=[lt_t])
            op("act", lambda e: e.activation(out=lt[:, 4:6], in_=lt[:, 0:2], func=AF.Exp),
               rd=[lt_t], wr=[lt_t])
            op("dve", lambda e: e.tensor_tensor(out=lt[:, 8:9], in0=lt[:, 5:6],
                                                in1=lt[:, 4:5], op=ALU.subtract),
               rd=[lt_t], wr=[lt_t])
            op("dve", lambda e: e.tensor_scalar(out=lt[:, 9:10], in0=lt[:, 8:9],
                                                scalar1=-LAM_INIT, scalar2=None, op0=ALU.add),
               rd=[lt_t], wr=[lt_t])

            def Bc(i):
                return cst[:, i, :]

            for b in range(nblk):
              try:
                t0 = b * T
                bi = b % 2
                xb, xb_t = xT[bi], xT_t[bi]
                cosq, sinq, cosk, sink = tabs
                st(0)
                op("sp", lambda e: e.dma_start(out=xb[:], in_=xTv[:, :, t0:t0 + T]),
                   wr=[xb_t], chan=xld[bi])
                op("sp", lambda e: e.dma_start(out=posi[:], in_=posd[:, t0:t0 + T]),
                   wr=[posi_t], chan=pch)
                norm_stats(xb, xb_t)
                norm_apply(xb, xb_t, P_GMIX, hT, hT_t)
                st(1)
                pf, pft = gettmp()
                ang, angt = gettmp()
                kf, kft = gettmp()
                r, rt = gettmp()
                m, mt = gettmp()
                sv, svt = gettmp()
                cv, cvt = gettmp()
                op("dve", lambda e: e.tensor_copy(out=pf[:, 0:T], in_=posi[:]), rd=[posi_t], wr=[pft])
                op("dve", lambda e: e.tensor_scalar(out=ang[:, 0:T], in0=pf[:, 0:T], scalar1=pcol(P_INVF),
                                                    scalar2=None, op0=ALU.mult),
                   rd=[pft, prm_t], wr=[angt])
                op("dve", lambda e: e.tensor_scalar(out=ki[:], in0=ang[:, 0:T], scalar1=1.0 / TWO_PI,
                                                    scalar2=None, op0=ALU.mult),
                   rd=[angt], wr=[ki_t])
                op("dve", lambda e: e.tensor_copy(out=kf[:, 0:T], in_=ki[:]), rd=[ki_t], wr=[kft])
                op("dve", lambda e: e.scalar_tensor_tensor(out=r[:, 0:T], in0=kf[:, 0:T], scalar=-C1,
                                                           in1=ang[:, 0:T], op0=ALU.mult, op1=ALU.add),
                   rd=[kft, angt], wr=[rt])
                op("dve", lambda e: e.scalar_tensor_tensor(out=r[:, 0:T], in0=kf[:, 0:T], scalar=-C2,
                                                           in1=r[:, 0:T], op0=ALU.mult, op1=ALU.add),
                   rd=[kft, rt], wr=[rt])
                op("dve", lambda e: e.tensor_scalar(out=m[:, 0:T], in0=r[:, 0:T], scalar1=math.pi,
                                                    scalar2=-TWO_PI, op0=ALU.is_gt, op1=ALU.mult),
                   rd=[rt], wr=[mt])
                op("dve", lambda e: e.tensor_tensor(out=r[:, 0:T], in0=r[:, 0:T], in1=m[:, 0:T], op=ALU.add),
                   rd=[rt, mt], wr=[rt])
                op("dve", lambda e: e.tensor_scalar(out=m[:, 0:T], in0=r[:, 0:T], scalar1=-math.pi,
                                                    scalar2=TWO_PI, op0=ALU.is_lt, op1=ALU.mult),
                   rd=[rt], wr=[mt])
                op("dve", lambda e: e.tensor_tensor(out=r[:, 0:T], in0=r[:, 0:T], in1=m[:, 0:T], op=ALU.add),
                   rd=[rt, mt], wr=[rt])
                op("act", lambda e: e.activation(out=sv[:, 0:T], in_=r[:, 0:T], func=AF.Sin),
                   rd=[rt], wr=[svt])
                op("act", lambda e: e.activation(out=m[:, 0:T], in_=r[:, 0:T], func=AF.Abs),
                   rd=[rt], wr=[mt])
                op("dve", lambda e: e.tensor_scalar(out=m[:, 0:T], in0=m[:, 0:T], scalar1=-1.0,
                                                    scalar2=math.pi / 2, op0=ALU.mult, op1=ALU.add),
                   rd=[mt], wr=[mt])
                op("act", lambda e: e.activation(out=cv[:, 0:T], in_=m[:, 0:T], func=AF.Sin),
                   rd=[mt], wr=[cvt])
                op("dve", lambda e: e.tensor_scalar(out=sink[:], in0=sv[:, 0:T], scalar1=pcol(P_SGN),
                                                    scalar2=None, op0=ALU.mult),
                   rd=[svt, prm_t], wr=[tab_t])
                op("dve", lambda e: e.tensor_scalar(out=sinq[:], in0=sink[:], scalar1=0.125,
                                                    scalar2=None, op0=ALU.mult),
                   rd=[tab_t], wr=[tab_t])
                op("dve", lambda e: e.tensor_copy(out=cosk[:], in_=cv[:, 0:T]), rd=[cvt], wr=[tab_t])
                op("dve", lambda e: e.tensor_scalar(out=cosq[:], in0=cv[:, 0:T], scalar1=0.125,
                                                    scalar2=None, op0=ALU.mult),
                   rd=[cvt], wr=[tab_t])

                st(2)
                for which in range(2):
                    for h in range(NH):
                        col0 = which * 512 + h * 128
                        bk, bt, _ = getbank()
                        for k in range(KC):
                            op("pe", lambda e, k=k: e.matmul(bk[:, 0:T], lhsT=Wi[:, k, col0:col0 + 128],
                                                             rhs=hT[:, k, :], start=(k == 0), stop=(k == KC - 1)),
                               rd=[Wi_t[k], hT_t], wr=[bt])
                        ri = (which * NH + h) % 2
                        op("act", lambda e: e.activation(out=raw[ri][:], in_=bk[:, 0:T], func=AF.Copy),
                           rd=[bt], wr=[raw_t[ri]])
                        bk2, bt2, _ = getbank()
                        op("pe", lambda e: e.matmul(bk2[:, 0:T], lhsT=Bc(C_PERM), rhs=raw[ri][:],
                                                    start=True, stop=True),
                           rd=[cst_t, raw_t[ri]], wr=[bt2])
                        ta, tat = gettmp()
                        tb, tbt = gettmp()
                        ctab, stab = (cosq, sinq) if which == 0 else (cosk, sink)
                        op("dve", lambda e: e.tensor_tensor(out=ta[:, 0:T], in0=bk[:, 0:T], in1=ctab[:], op=ALU.mult),
                           rd=[bt, tab_t], wr=[tat])
                        op("dve", lambda e: e.tensor_tensor(out=tb[:, 0:T], in0=bk2[:, 0:T], in1=stab[:], op=ALU.mult),
                           rd=[bt2, tab_t], wr=[tbt])
                        if which == 0:
                            for n in range(2):
                                pr = slice(64 * n, 64 * n + 64)
                                op("dve", lambda e, n=n, pr=pr: e.tensor_tensor(
                                    out=qm[n][pr, h, :], in0=ta[pr, 0:T], in1=tb[pr, 0:T], op=ALU.add),
                                   rd=[tat, tbt], wr=[qT_t[h]])
                        else:
                            dst, dtk = KT[:, h, t0:t0 + T], KT_t[h][b]
                            op("dve", lambda e: e.tensor_tensor(out=dst, in0=ta[:, 0:T], in1=tb[:, 0:T], op=ALU.add),
                               rd=[tat, tbt], wr=[dtk])
                st(3)
                for tt in range(T // 128):
                    for which in range(2):
                        bk, bt, _ = getbank()
                        c0 = 1024 + which * 512
                        for k in range(KC):
                            op("pe", lambda e, k=k: e.matmul(bk[:, :], lhsT=hT[:, k, tt * 128:(tt + 1) * 128],
                                                             rhs=Wi[:, k, c0:c0 + 512],
                                                             start=(k == 0), stop=(k == KC - 1)),
                               rd=[Wi_t[k], hT_t], wr=[bt])
                        if which == 0:
                            vi = b * (T // 128) + tt
                            op("act", lambda e: e.activation(out=V[:, vi, :], in_=bk[:, :], func=AF.Copy),
                               rd=[bt], wr=[V_t[vi]])
                        else:
                            op("act", lambda e: e.activation(out=ptok[:, 1 + tt, :], in_=bk[:, :], func=AF.Copy),
                               rd=[bt], wr=[ptok_t[1 + tt]])
                st(4)
                for g in range(4):
                    bk, bt, _ = getbank()
                    for tt in range(T // 128):
                        first = (b == 0 and tt == 0)
                        op("pe", lambda e: e.matmul(bk[:, tt * 128:(tt + 1) * 128],
                                                    lhsT=ptok[:, tt, g * 128:(g + 1) * 128],
                                                    rhs=Bc(C_BOFF + g), start=True, stop=False),
                           rd=[ptok_t[tt], cst_t], wr=[bt])
                        op("pe", lambda e: e.matmul(bk[:, tt * 128:(tt + 1) * 128],
                                                    lhsT=ptok[:, tt + 1, g * 128:(g + 1) * 128],
                                                    rhs=Bc((C_BD0 if first else C_BD) + g), start=False, stop=True),
                           rd=[ptok_t[tt + 1], cst_t], wr=[bt])
                    op("act", lambda e: e.activation(out=dT[:, g, :], in_=bk[:, 0:T], func=AF.Copy),
                       rd=[bt], wr=[dT_t[g]])
                    bk2, bt2, _ = getbank()
                    op("pe", lambda e: e.matmul(bk2[:, 0:T], lhsT=Pw[:, g, :], rhs=dT[:, g, :],
                                                start=True, stop=True),
                       rd=[Pw_t, dT_t[g]], wr=[bt2])
                    op("dve", lambda e: e.tensor_scalar(out=catT[:, g, :], in0=bk2[:, 0:T],
                                                        scalar1=pcol(P_PSC + g), scalar2=None, op0=ALU.mult),
                       rd=[bt2, prm_t], wr=[cat_t[g]])
                op("dve", lambda e: e.tensor_copy(out=ptok[:, 0, :], in_=ptok[:, 2, :]),
                   rd=[ptok_t[2]], wr=[ptok_t[0]])

                st(5)
                nk = 2 * b + 2
                for h in range(NH):
                    bo0, bo0t, io0 = getbank(hold=True)
                    bo1, bo1t, io1 = getbank(hold=True)
                    bsm, bsmt, ism = getbank(hold=True)
                    bos = (bo0, bo1)
                    bots = (bo0t, bo1t)
                    sbk = {}

                    def qk(j):
                        bk, bt, _ = getbank()
                        sbk[j] = (bk, bt)
                        jj = j - 2 * b
                        c0 = 128 * jj if jj >= 0 else 0
                        for n in range(2):
                            op("pe", lambda e, n=n: e.matmul(
                                bk[:, n * T + c0:(n + 1) * T],
                                lhsT=KT[:, h, j * 128:(j + 1) * 128], rhs=qm[n][:, h, c0:T],
                                start=True, stop=True),
                               rd=[KT_t[h][j // 2], qT_t[h]], wr=[bt])
                        return c0

                    c0s = {0: qk(0)}
                    for j in range(nk):
                        if j + 1 < nk:
                            c0s[j + 1] = qk(j + 1)
                        bk, bt = sbk.pop(j)
                        c0 = c0s[j]
                        P, Pt = Pb[j % 3], Pb_t[j % 3]
                        op("act", lambda e: e.activation(
                            out=P[:, :, c0:T],
                            in_=bk[:, 0:2 * T].rearrange("p (n t) -> p n t", n=2)[:, :, c0:T], func=AF.Exp),
                           rd=[bt], wr=[Pt])
                        if j >= 2 * b:
                            op("dve", lambda e: e.memset(P[64:128, :, c0:c0 + 64], 0.0), wr=[Pt])
                        vi = j
                        for n in range(2):
                            op("pe", lambda e, n=n: e.matmul(bos[n][:, c0:T], lhsT=V[:, vi, h * 128:(h + 1) * 128],
                                                             rhs=P[:, n, c0:T], start=(j == 0), stop=(j == nk - 1)),
                               rd=[V_t[vi], Pt], wr=[bots[n]])
                        for n in range(2):
                            op("pe", lambda e, n=n: e.matmul(bsm[:, n * T + c0:(n + 1) * T], lhsT=ones_b[:, :],
                                                             rhs=P[:, n, c0:T], start=(j == 0 and n == 0),
                                                             stop=(j == nk - 1 and n == 1), skip_group_check=True),
                               rd=[ones_t, Pt], wr=[bsmt])
                    lns, lnst = rrb[0], rrb_t[0]
                    rr, rrt = rrb[1], rrb_t[1]
                    op("act", lambda e: e.activation(out=lns[:], in_=bsm[:, :], func=AF.Ln),
                       rd=[bsmt], wr=[lnst])
                    op("act", lambda e: e.activation(out=rr[:], in_=lns[:], func=AF.Exp, scale=-1.0),
                       rd=[lnst], wr=[rrt])
                    held.discard(ism)
                    u0, u0t = gettmp()
                    u1, u1t = gettmp()
                    op("dve", lambda e: e.tensor_tensor(out=u0[:, 0:T], in0=bo0[:, 0:T], in1=rr[:, 0:T], op=ALU.mult),
                       rd=[bo0t, rrt], wr=[u0t])
                    op("dve", lambda e: e.scalar_tensor_tensor(out=u1[:, 0:T], in0=bo1[:, 0:T], scalar=lt[:, 9:10],
                                                               in1=rr[:, T:2 * T], op0=ALU.mult, op1=ALU.mult),
                       rd=[bo1t, rrt, lt_t], wr=[u1t])
                    held.discard(io0)
                    held.discard(io1)
                    op("dve", lambda e: e.tensor_tensor(out=u0[:, 0:T], in0=u0[:, 0:T], in1=u1[:, 0:T], op=ALU.add),
                       rd=[u0t, u1t], wr=[u0t])
                    op("act", lambda e: e.activation(out=osqb[:], in_=u0[:, 0:T], func=AF.Square),
                       rd=[u0t], wr=[osqb_t])
                    bst, bstt, _ = getbank()
                    op("pe", lambda e: e.matmul(bst[:, 0:T], lhsT=ones_b[:], rhs=osqb[:], start=True, stop=True),
                       rd=[ones_t, osqb_t], wr=[bstt])
                    l2, l2t = gettmp()
                    op("act", lambda e: e.activation(out=l2[:, 0:T], in_=bst[:, 0:T], func=AF.Ln,
                                                     scale=1.0 / 128, bias=EPS),
                       rd=[bstt], wr=[l2t])
                    rn, rnt = gettmp()
                    op("act", lambda e: e.activation(out=rn[:, 0:T], in_=l2[:, 0:T], func=AF.Exp, scale=-0.5,
                                                     bias=math.log(1.0 - LAM_INIT)),
                       rd=[l2t], wr=[rnt])
                    op("dve", lambda e: e.scalar_tensor_tensor(out=catT[:, 4 + h, :], in0=u0[:, 0:T],
                                                               scalar=pcol(P_HG + h), in1=rn[:, 0:T],
                                                               op0=ALU.mult, op1=ALU.mult),
                       rd=[u0t, rnt, prm_t], wr=[cat_t[4 + h]])

                st(6)
                for c in range(KC):
                    bk, bt, _ = getbank()
                    for k in range(KC):
                        op("pe", lambda e, k=k: e.matmul(bk[:, 0:T], lhsT=Wo[:, k, c * 128:(c + 1) * 128],
                                                         rhs=catT[:, k, :], start=(k == 0), stop=(k == KC - 1)),
                           rd=[Wo_t[k], cat_t[k]], wr=[bt])
                    op("dve", lambda e: e.tensor_tensor(out=xb[:, c, :], in0=xb[:, c, :], in1=bk[:, 0:T], op=ALU.add),
                       rd=[bt, xb_t], wr=[xb_t])
                st(7)
                op("sp", lambda e: e.dma_start(out=x1v[:, :, t0:t0 + T], in_=xb[:]),
                   rd=[xb_t], wr=[x1_t[b]], chan=xst[bi])
              except _Stop:
                break

            if dbg is not None and dbg < 0:
                pass
            for en in ("sp", "pool", "pe", "act", "dve"):
                sy.wait_all(en, list(sy.allchans))

        with ExitStack() as p2:
          try:
            st(8)
            def sb2(name, shape, dt):
                return sb(name, shape, dt, p2)
            Wu = sb2("Wu", [128, 44, KC * 128], BF16)
            Wu_t = [Tok() for _ in range(11)]
            Wd = sb2("Wd", [128, FC, 1024], BF16)
            Wd_t = [Tok() for _ in range(11)]
            actT = sb2("actT", [128, FC, T], BF16)
            act_t = [Tok() for _ in range(FC)]
            halo = sb2("halo", [128, 44, 2], F32)
            halo_t = [Tok() for _ in range(44)]
            wch = sy.dma_chan()
            wuv = wud.rearrange("p (g n) -> p g n", g=11)
            wdv = wdd.rearrange("p (g n) -> p g n", g=11)
            for g in range(11):
                op("pool", lambda e, g=g: e.dma_start(
                    out=Wu[:, 4 * g:4 * g + 4, :], in_=wuv[:, g, :].rearrange("p (a b) -> p a b", a=4)),
                   wr=[Wu_t[g]], chan=wch)
            for g in range(11):
                op("pool", lambda e, g=g: e.dma_start(
                    out=Wd[:, 2 * g:2 * g + 2, :], in_=wdv[:, g, :].rearrange("p (a b) -> p a b", a=2)),
                   wr=[Wd_t[g]], chan=wch)
            for _t in Wu_t + Wd_t:
                _t.w = (wch, wch.cnt)
            op("dve", lambda e: e.memset(halo[:], 0.0), wr=halo_t)

            for b in range(nblk):
                t0 = b * T
                bi = b % 2
                xb, xb_t = xT[bi], xT_t[bi]
                op("sp", lambda e: e.dma_start(out=xb[:], in_=x1v[:, :, t0:t0 + T]),
                   rd=[x1_t[b]], wr=[xb_t], chan=xld[bi])
                norm_stats(xb, xb_t)
                norm_apply(xb, xb_t, P_GFFN, hT, hT_t)
                st(9)
                for i in range(FC):
                    tvs = []
                    for gv in range(2):
                        ci = gv * FC + i
                        bk, bt, _ = getbank()
                        for k in range(KC):
                            op("pe", lambda e, k=k: e.matmul(bk[:, 0:T], lhsT=Wu[:, ci, k * 128:(k + 1) * 128],
                                                             rhs=hT[:, k, :], start=(k == 0), stop=(k == KC - 1)),
                               rd=[Wu_t[ci // 4], hT_t], wr=[bt])
                        ub, ubt = gettmp()
                        tv, tvt = gettmp()
                        op("pool", lambda e: e.tensor_copy(out=ub[:, 0:2], in_=halo[:, ci, :]),
                           rd=[halo_t[ci]], wr=[ubt])
                        op("act", lambda e: e.activation(out=ub[:, 2:T + 2], in_=bk[:, 0:T], func=AF.Copy),
                           rd=[bt], wr=[ubt])
                        op("pool", lambda e: e.tensor_copy(out=halo[:, ci, :], in_=ub[:, T:T + 2]),
                           rd=[ubt], wr=[halo_t[ci]])
                        op("pool", lambda e: e.tensor_scalar(out=tv[:, 0:T], in0=ub[:, 2:T + 2],
                                                             scalar1=pcol(P_CW2 + ci), scalar2=pcol(P_CB + ci),
                                                             op0=ALU.mult, op1=ALU.add),
                           rd=[ubt, prm_t], wr=[tvt])
                        op("dve", lambda e: e.scalar_tensor_tensor(out=tv[:, 0:T], in0=ub[:, 1:T + 1],
                                                                   scalar=pcol(P_CW1 + ci), in1=tv[:, 0:T],
                                                                   op0=ALU.mult, op1=ALU.add),
                           rd=[ubt, tvt, prm_t], wr=[tvt])
                        op("dve", lambda e: e.scalar_tensor_tensor(out=tv[:, 0:T], in0=ub[:, 0:T],
                                                                   scalar=pcol(P_CW0 + ci), in1=tv[:, 0:T],
                                                                   op0=ALU.mult, op1=ALU.add),
                           rd=[ubt, tvt, prm_t], wr=[tvt])
                        tvs.append((tv, tvt))
                    (tg, tgt), (tvv, tvvt) = tvs
                    sg, sgt = gettmp()
                    op("act", lambda e: e.activation(out=sg[:, 0:T], in_=tg[:, 0:T], func=AF.Silu),
                       rd=[tgt], wr=[sgt])
                    op("dve", lambda e: e.tensor_tensor(out=actT[:, i, :], in0=sg[:, 0:T], in1=tvv[:, 0:T], op=ALU.mult),
                       rd=[sgt, tvvt], wr=[act_t[i]])
                st(10)
                for c in range(KC):
                    bk, bt, _ = getbank()
                    for f in range(FC):
                        op("pe", lambda e, f=f: e.matmul(bk[:, 0:T], lhsT=Wd[:, f, c * 128:(c + 1) * 128],
                                                         rhs=actT[:, f, :], start=(f == 0), stop=(f == FC - 1)),
                           rd=[Wd_t[f // 2], act_t[f]], wr=[bt])
                    op("dve", lambda e: e.tensor_tensor(out=xb[:, c, :], in0=xb[:, c, :], in1=bk[:, 0:T], op=ALU.add),
                       rd=[bt, xb_t], wr=[xb_t])
                norm_stats(xb, xb_t)
                norm_apply(xb, xb_t, P_GFIN, xb, xb_t)
                op("sp", lambda e: e.dma_start(out=outv[:, :, t0:t0 + T], in_=xb[:]),
                   rd=[xb_t], wr=[Tok()], chan=xst[bi])

          except _Stop:
            pass
        for en in ("sp", "pool", "pe", "act", "dve"):
            sy.wait_all(en, sy.allchans)
    if marks is None:
        return {k: sorted(v) for k, v in sy.needed.items()}
    return nc


def _consts():
    wins = (2, 4, 8, 16)
    cst = np.zeros((128, NCST, 128), np.float32)
    s = np.arange(128)[:, None]
    t = np.arange(128)[None, :]
    for g, w in enumerate(wins):
        inwin = (s <= t) & (s > t - w)
        cnt0 = np.minimum(t + 1, w).astype(np.float32)
        cst[:, C_BD0 + g, :] = np.where(inwin, 1.0 / cnt0, 0.0) - (s == t)
        cst[:, C_BD + g, :] = np.where(inwin, 1.0 / w, 0.0) - (s == t)
        cst[:, C_BOFF + g, :] = np.where((s - 128) > (t - w), 1.0 / w, 0.0)
    m = np.arange(128)
    src = np.where((m % 64) < 32, m + 32, m - 32)
    perm = np.zeros((128, 128), np.float32)
    perm[src, m] = 1.0
    cst[:, C_PERM, :] = perm
    return cst.reshape(128, NCST * 128)


def _kmaj(w, kc):
    n = w.shape[1]
    return np.ascontiguousarray(w.reshape(kc, 128, n).transpose(1, 0, 2).reshape(128, kc * n))


def _cols(v, nchunk):
    return np.ascontiguousarray(v.reshape(nchunk, 128).T)


def kernel(x, positions, norm_mix_g, w_in, pool_w, pool_scale, lambda_q1, lambda_k1,
           lambda_q2, lambda_k2, attn_norm_g, w_o, norm_ffn_g, w_up, conv_w, conv_b,
           w_down, norm_final_g, _nblk=NB, _trace=False, _stop=None, _prep_only=False):
    f32 = np.float32
    x = np.asarray(x, f32)
    positions = np.asarray(positions, np.int32)
    w_in0 = np.asarray(w_in, f32)[0]
    wi = np.concatenate([w_in0[:, 512:1024], w_in0[:, 1024:1536], w_in0[:, 1536:2048], w_in0[:, 0:512]], axis=1)
    wi = _kmaj(wi, KC)
    wo = _kmaj(np.asarray(w_o, f32)[0], KC)
    pw = np.ascontiguousarray(np.asarray(pool_w, f32)[0].transpose(1, 0, 2).reshape(128, 512))
    wu0 = np.asarray(w_up, f32)[0]
    wu = np.ascontiguousarray(wu0.reshape(KC, 128, 44, 128).transpose(1, 2, 0, 3).reshape(128, 44 * KC * 128))
    wd = _kmaj(np.asarray(w_down, f32)[0], FC)
    prm = np.zeros((128, NPRM), f32)
    prm[:, P_GMIX:P_GMIX + 8] = _cols(np.asarray(norm_mix_g, f32)[0], 8)
    prm[:, P_GFFN:P_GFFN + 8] = _cols(np.asarray(norm_ffn_g, f32)[0], 8)
    prm[:, P_GFIN:P_GFIN + 8] = _cols(np.asarray(norm_final_g, f32), 8)
    prm[:, P_PSC:P_PSC + 4] = _cols(np.asarray(pool_scale, f32)[0], 4)
    prm[:, P_HG:P_HG + 4] = np.asarray(attn_norm_g, f32)[0].T
    cw = np.asarray(conv_w, f32)[0]
    prm[:, P_CW0:P_CW0 + 44] = _cols(cw[0], 44)
    prm[:, P_CW1:P_CW1 + 44] = _cols(cw[1], 44)
    prm[:, P_CW2:P_CW2 + 44] = _cols(cw[2], 44)
    prm[:, P_CB:P_CB + 44] = _cols(np.asarray(conv_b, f32)[0], 44)
    invf = (np.float32(10000.0) ** (-np.arange(0, 64, 2, dtype=f32) / np.float32(64))).astype(f32)
    p = np.arange(128)
    prm[:, P_INVF] = invf[p % 32]
    prm[:, P_SGN] = np.where((p % 64) < 32, -1.0, 1.0)
    lamv = np.concatenate([np.asarray(lambda_q1, f32)[0], np.asarray(lambda_q2, f32)[0],
                           np.asarray(lambda_k1, f32)[0], np.asarray(lambda_k2, f32)[0]]).reshape(1, 256)
    lamv = np.ascontiguousarray(np.broadcast_to(lamv, (128, 256)))
    cst = _consts()
    if _prep_only:
        b = 0
        return {
            "xT": np.ascontiguousarray(x[b].T),
            "pos": np.ascontiguousarray(np.broadcast_to(positions[b][None, :], (128, S))),
            "wi": wi, "wo": wo, "pw": pw, "wu": wu, "wd": wd, "prm": prm, "lamv": lamv, "cst": cst,
        }
    nc = build_nc(nblk=_nblk, stop=_stop)
    in_maps = []
    for b in range(8):
        in_maps.append({
            "xT": np.ascontiguousarray(x[b].T),
            "pos": np.ascontiguousarray(np.broadcast_to(positions[b][None, :], (128, S))),
            "wi": wi, "wo": wo, "pw": pw, "wu": wu, "wd": wd, "prm": prm, "lamv": lamv, "cst": cst,
        })
    res = run_bass_kernel_spmd(nc, in_maps, core_ids=list(range(8)), **({"trace": True} if _trace else {}))
    if _trace:
        print("exec_time_ns", res.exec_time_ns)
    out = np.stack([np.ascontiguousarray(r["outT"].T) for r in res.results], axis=0)
    return out.astype(np.float32)
```
